# Optimizing a Trainium2 kernel written in Bass

```python
import math
import jax
import jax.numpy as jnp
from jax import lax
import numpy as np

D_MODEL = 2048
BATCH = 2
SEQ = 16384
DEPTH = 2

GRID_W = 64
CTX_LEN = 256

NA_HEADS = 8
NA_HEAD_DIM = 128
NA_WIDTH = NA_HEADS * NA_HEAD_DIM
NA_KH_MAX = 8
NA_KW = 16

FT_GROUPS = 4
FT_GROUP_DIM = 128
FT_WIDTH = FT_GROUPS * FT_GROUP_DIM

POOL_WINDOWS = (2, 4, 8, 16)
POOL_GROUP_DIM = 128
POOL_WIDTH = len(POOL_WINDOWS) * POOL_GROUP_DIM

SC_WIDTH = 512
SC_CONV = 3

N_BRANCH = 4
IN_WIDTH = 3 * NA_WIDTH + FT_WIDTH + POOL_WIDTH + 3 * SC_WIDTH
D_FF = 256 * math.ceil(8 * D_MODEL / (3 * 256))
N_MOD = 6
RMS_EPS = 1e-6

kernel_name = 'hybrid_natten_fnet_pool_shortconv_dit'


def rms_norm(x, w):
    xf = x.astype(jnp.float32)
    y = xf * lax.rsqrt(jnp.mean(xf * xf, axis=-1, keepdims=True) + RMS_EPS)
    return (y * w.astype(jnp.float32)).astype(x.dtype)


def modulate(h, shift, scale):
    return h * (1 + scale) + shift


def in_split_points():
    widths = (NA_WIDTH, NA_WIDTH, NA_WIDTH, FT_WIDTH, POOL_WIDTH, SC_WIDTH, SC_WIDTH)
    return [int(s) for s in np.cumsum(widths)]


def mixer_inputs(h, w_in, q_norm_w, k_norm_w):
    b, l, _ = h.shape
    q, k, v, ft, pool, sv, sb, sc = jnp.split(h @ w_in, in_split_points(), axis=-1)
    q = rms_norm(q.reshape(b, l, NA_HEADS, NA_HEAD_DIM), q_norm_w)
    k = rms_norm(k.reshape(b, l, NA_HEADS, NA_HEAD_DIM), k_norm_w)
    v = v.reshape(b, l, NA_HEADS, NA_HEAD_DIM)
    return q, k, v, ft, pool, sv, sb, sc


def context_kv(hc, w_in, k_norm_w):
    b, l, _ = hc.shape
    k = (hc @ w_in[:, NA_WIDTH:2 * NA_WIDTH]).reshape(b, l, NA_HEADS, NA_HEAD_DIM)
    v = (hc @ w_in[:, 2 * NA_WIDTH:3 * NA_WIDTH]).reshape(b, l, NA_HEADS, NA_HEAD_DIM)
    return rms_norm(k, k_norm_w), v


def context_attention(q, k, v):
    s = jnp.einsum('blhd,bmhd->bhlm', q, k, preferred_element_type=jnp.float32) * (NA_HEAD_DIM ** -0.5)
    p = jax.nn.softmax(s, axis=-1).astype(v.dtype)
    o = jnp.einsum('bhlm,bmhd->blhd', p, v)
    return o.reshape(q.shape[0], q.shape[1], NA_WIDTH)


def neighbourhood_attention(q, k, v, k_ctx, v_ctx, rpb):
    b, s, h, dh = q.shape
    rows = s // GRID_W
    kh = min(NA_KH_MAX, rows)
    qg = q.reshape(b, rows, GRID_W, h, dh)
    kg = k.reshape(b, rows, GRID_W, h, dh)
    vg = v.reshape(b, rows, GRID_W, h, dh)
    cols = np.arange(GRID_W)
    col_start = np.clip(cols - NA_KW // 2, 0, GRID_W - NA_KW)
    col_idx = col_start[:, None] + np.arange(NA_KW)[None, :]
    col_off = col_idx - cols[:, None] + (NA_KW - 1)
    bias_cols = jnp.transpose(rpb[:, :, col_off], (0, 2, 1, 3))
    scale = dh ** -0.5

    def row_block(r):
        rs = jnp.clip(r - kh // 2, 0, rows - kh)
        k_win = lax.dynamic_slice_in_dim(kg, rs, kh, axis=1)[:, :, col_idx]
        v_win = lax.dynamic_slice_in_dim(vg, rs, kh, axis=1)[:, :, col_idx]
        q_r = lax.dynamic_index_in_dim(qg, r, axis=1, keepdims=False)
        s_loc = jnp.einsum('bjhd,bajwhd->bhjaw', q_r, k_win, preferred_element_type=jnp.float32) * scale
        row_off = rs + jnp.arange(kh) - r + (NA_KH_MAX - 1)
        s_loc = s_loc + bias_cols[:, :, row_off].astype(jnp.float32)[None]
        s_ctx = jnp.einsum('bjhd,blhd->bhjl', q_r, k_ctx, preferred_element_type=jnp.float32) * scale
        s_all = jnp.concatenate([s_loc.reshape(b, h, GRID_W, kh * NA_KW), s_ctx], axis=-1)
        p = jax.nn.softmax(s_all, axis=-1).astype(v.dtype)
        p_loc = p[..., :kh * NA_KW].reshape(b, h, GRID_W, kh, NA_KW)
        p_ctx = p[..., kh * NA_KW:]
        return jnp.einsum('bhjaw,bajwhd->bjhd', p_loc, v_win) + jnp.einsum('bhjl,blhd->bjhd', p_ctx, v_ctx)

    out = lax.map(row_block, jnp.arange(rows))
    return jnp.transpose(out, (1, 0, 2, 3, 4)).reshape(b, s, h * dh)


def fourier_mix(u):
    b, l, _ = u.shape
    ug = u.astype(jnp.float32).reshape(b, l, FT_GROUPS, FT_GROUP_DIM)
    f = jnp.fft.fft2(ug, axes=(1, 3), norm='ortho').real
    return f.reshape(b, l, FT_WIDTH).astype(u.dtype)


def pool_mix(u, w_pool, pool_scale):
    b, l, _ = u.shape
    uf = u.astype(jnp.float32)
    cs = jnp.pad(jnp.cumsum(uf, axis=1), ((0, 0), (1, 0), (0, 0)))
    pos = jnp.arange(l)
    outs = []
    for g, w in enumerate(POOL_WINDOWS):
        lo = jnp.clip(pos - w // 2, 0, l)
        hi = jnp.clip(pos - w // 2 + w, 0, l)
        c0, c1 = g * POOL_GROUP_DIM, (g + 1) * POOL_GROUP_DIM
        win_sum = cs[:, hi, c0:c1] - cs[:, lo, c0:c1]
        cnt = (hi - lo).astype(jnp.float32)[None, :, None]
        outs.append(win_sum / cnt - uf[:, :, c0:c1])
    p = jnp.stack(outs, axis=2)
    y = jnp.einsum('blgc,gcd->blgd', p, w_pool.astype(jnp.float32)).reshape(b, l, POOL_WIDTH)
    return (y * pool_scale.astype(jnp.float32)).astype(u.dtype)


def short_conv_mix(sv, sb, sc, conv_w):
    u = sc * sv
    up = jnp.pad(u, ((0, 0), (1, 1), (0, 0)))
    y = up[:, :-2] * conv_w[0] + up[:, 1:-1] * conv_w[1] + up[:, 2:] * conv_w[2]
    return sb * y


def merge_branches(h, branches, w_gate, w_branch, w_o):
    y = jax.nn.sigmoid(h @ w_gate[0]) * (branches[0] @ w_branch[0])
    for i in range(1, N_BRANCH):
        y = y + jax.nn.sigmoid(h @ w_gate[i]) * (branches[i] @ w_branch[i])
    return y @ w_o


def swiglu(h, w_g, w_u, w_d):
    return (jax.nn.silu(h @ w_g) * (h @ w_u)) @ w_d


def setup_inputs(seed: int = 0) -> dict:
    key = jax.random.key(seed)
    ks = jax.random.split(key, 32)
    f32 = jnp.float32
    L = DEPTH
    D = D_MODEL

    def nrm(k, shape, scale):
        return jax.random.normal(k, shape, f32) * scale

    return {
        'x': nrm(ks[0], (BATCH, SEQ, D), 1.0),
        'c': nrm(ks[1], (BATCH, D), 1.0),
        'ctx': nrm(ks[2], (BATCH, CTX_LEN, D), 1.0),
        'c_ctx': nrm(ks[3], (D,), 1.0),
        'w_mod': nrm(ks[4], (L, D, N_MOD * D), D ** -0.5),
        'b_mod': nrm(ks[5], (L, N_MOD * D), 0.01),
        'norm1_w': 1.0 + nrm(ks[6], (L, D), 0.05),
        'w_in': nrm(ks[7], (L, D, IN_WIDTH), D ** -0.5),
        'q_norm_w': 1.0 + nrm(ks[8], (L, NA_HEAD_DIM), 0.05),
        'k_norm_w': 1.0 + nrm(ks[9], (L, NA_HEAD_DIM), 0.05),
        'rpb': nrm(ks[10], (L, NA_HEADS, 2 * NA_KH_MAX - 1, 2 * NA_KW - 1), 0.5),
        'w_pool': nrm(ks[11], (L, len(POOL_WINDOWS), POOL_GROUP_DIM, POOL_GROUP_DIM), POOL_GROUP_DIM ** -0.5),
        'pool_scale': 1.0 + nrm(ks[12], (L, POOL_WIDTH), 0.05),
        'conv_w': nrm(ks[13], (L, SC_CONV, SC_WIDTH), SC_CONV ** -0.5),
        'w_gate': nrm(ks[14], (L, N_BRANCH, D, D), D ** -0.5),
        'w_pa': nrm(ks[15], (L, NA_WIDTH, D), NA_WIDTH ** -0.5),
        'w_pb': nrm(ks[16], (L, FT_WIDTH, D), FT_WIDTH ** -0.5),
        'w_pc': nrm(ks[17], (L, POOL_WIDTH, D), POOL_WIDTH ** -0.5),
        'w_pd': nrm(ks[18], (L, SC_WIDTH, D), SC_WIDTH ** -0.5),
        'w_o': nrm(ks[19], (L, D, D), D ** -0.5),
        'norm2_w': 1.0 + nrm(ks[20], (L, D), 0.05),
        'w_ffn_gate': nrm(ks[21], (L, D, D_FF), D ** -0.5),
        'w_ffn_up': nrm(ks[22], (L, D, D_FF), D ** -0.5),
        'w_ffn_down': nrm(ks[23], (L, D_FF, D), D_FF ** -0.5),
    }


def reference(x, c, ctx, c_ctx, w_mod, b_mod, norm1_w, w_in, q_norm_w, k_norm_w, rpb, w_pool,
              pool_scale, conv_w, w_gate, w_pa, w_pb, w_pc, w_pd, w_o, norm2_w, w_ffn_gate,
              w_ffn_up, w_ffn_down):
    xc = ctx
    for i in range(DEPTH):
        w_branch = (w_pa[i], w_pb[i], w_pc[i], w_pd[i])
        mod = jnp.split(jax.nn.silu(c) @ w_mod[i] + b_mod[i], N_MOD, axis=-1)
        sh1, sc1, g1, sh2, sc2, g2 = [m[:, None, :] for m in mod]
        csh1, csc1, cg1, csh2, csc2, cg2 = jnp.split(jax.nn.silu(c_ctx) @ w_mod[i] + b_mod[i], N_MOD, axis=-1)

        hc = modulate(rms_norm(xc, norm1_w[i]), csh1, csc1)
        if i == DEPTH - 1:
            kc, vc = context_kv(hc, w_in[i], k_norm_w[i])
        else:
            qc, kc, vc, ftc, poolc, svc, sbc, scc = mixer_inputs(hc, w_in[i], q_norm_w[i], k_norm_w[i])
            branches_c = (context_attention(qc, kc, vc), fourier_mix(ftc),
                          pool_mix(poolc, w_pool[i], pool_scale[i]), short_conv_mix(svc, sbc, scc, conv_w[i]))
            xc = xc + cg1 * merge_branches(hc, branches_c, w_gate[i], w_branch, w_o[i])
            xc = xc + cg2 * swiglu(modulate(rms_norm(xc, norm2_w[i]), csh2, csc2),
                                   w_ffn_gate[i], w_ffn_up[i], w_ffn_down[i])

        h = modulate(rms_norm(x, norm1_w[i]), sh1, sc1)
        q, k, v, ft, pool, sv, sb, sc = mixer_inputs(h, w_in[i], q_norm_w[i], k_norm_w[i])
        branches = (neighbourhood_attention(q, k, v, kc, vc, rpb[i]), fourier_mix(ft),
                    pool_mix(pool, w_pool[i], pool_scale[i]), short_conv_mix(sv, sb, sc, conv_w[i]))
        x = x + g1 * merge_branches(h, branches, w_gate[i], w_branch, w_o[i])
        x = x + g2 * swiglu(modulate(rms_norm(x, norm2_w[i]), sh2, sc2),
                            w_ffn_gate[i], w_ffn_up[i], w_ffn_down[i])
    return x
```

```python
import contextlib
import math
import numpy as np
import ml_dtypes
import concourse.bass as bass
import concourse.mybir as mybir
from concourse.bass_utils import run_bass_kernel_spmd

F32 = mybir.dt.float32
BF16 = mybir.dt.bfloat16
I32 = mybir.dt.int32
U32 = mybir.dt.uint32
AF = mybir.ActivationFunctionType
ALU = mybir.AluOpType
AX = mybir.AxisListType

D = 2048
KC = 16
L_DEPTH = 2
SEQ = 16384
NCORE = 8
OWN = 4096
HALO = 256
EXT = OWN + 2 * HALO
TT = 512
NTILE = OWN // TT
CTX = 256
NH = 8
DH = 128
INW = 5632
DFF = 5632
FKC = DFF // 128
EPS = 1e-6
NEG = -30000.0


class T:
    __slots__ = ("name", "w", "r")

    def __init__(self, name=""):
        self.name = name
        self.w = None
        self.r = {}


class Sched:
    CE = ("pe", "act", "dve", "pool")

    def __init__(self, nc, stack, n_dma_sems=36):
        self.nc = nc
        self.eng = dict(pe=nc.tensor, act=nc.scalar, dve=nc.vector, pool=nc.gpsimd, sp=nc.sync)
        self.sems = {}
        self.val = {}
        for e in self.CE:
            self.sems[e] = stack.enter_context(nc.semaphore("s_" + e))
            self.val[e] = 0
        self.dma_keys = {}
        self.dma_next = {}
        for q, n in (("sp", n_dma_sems), ("pool", 20), ("act", 24)):
            ks = []
            for i in range(n):
                k = "d_%s_%d" % (q, i)
                self.sems[k] = stack.enter_context(nc.semaphore(k))
                self.val[k] = 0
                ks.append(k)
            self.dma_keys[q] = ks
            self.dma_next[q] = 0
        self.cc_keys = []
        for i in range(12):
            k = "cc_%d" % i
            self.sems[k] = stack.enter_context(nc.semaphore(k))
            self.val[k] = 0
            self.cc_keys.append(k)
        self.cc_next = 0
        self.seen = {e: {} for e in self.eng}
        self.n_inst = 0
        self.n_wait = 0

    def _need(self, e, deps, key, val):
        if val <= 0 or self.seen[e].get(key, 0) >= val:
            return
        if deps.get(key, 0) < val:
            deps[key] = val

    def _collect(self, e, reads, writes, pe_accum=False):
        deps = {}
        for t in reads:
            if t.w is not None:
                self._need(e, deps, t.w[0], t.w[1])
        for t in writes:
            if t.w is not None:
                if not (pe_accum and t.w[0] == "pe"):
                    self._need(e, deps, t.w[0], t.w[1])
            for k, v in t.r.items():
                self._need(e, deps, k, v)
        return deps

    def _emit_waits(self, e, deps):
        for k, v in deps.items():
            self.eng[e].wait_ge(self.sems[k], v)
            self.seen[e][k] = v
            self.n_wait += 1

    def _mark(self, key, v, reads, writes):
        for t in reads:
            if t.r.get(key, 0) < v:
                t.r[key] = v
        for t in writes:
            t.w = (key, v)
            t.r = {}

    def op(self, e, fn, reads=(), writes=(), pe_accum=False):
        deps = self._collect(e, reads, writes, pe_accum)
        self._emit_waits(e, deps)
        inst = fn(self.eng[e])
        self.val[e] += 1
        inst.then_inc(self.sems[e], 1)
        self._mark(e, self.val[e], reads, writes)
        self.n_inst += 1
        return inst

    def dma(self, q, out, in_, reads=(), writes=(), indirect=None, **kw):
        ks = self.dma_keys[q]
        k = ks[self.dma_next[q] % len(ks)]
        self.dma_next[q] += 1
        deps = self._collect(q, reads, writes)
        self._need(q, deps, k, self.val[k])
        self._emit_waits(q, deps)
        if indirect is not None:
            inst = self.eng[q].indirect_dma_start(out=out, out_offset=None, in_=in_, in_offset=indirect, **kw)
        else:
            inst = self.eng[q].dma_start(out=out, in_=in_, **kw)
        self.val[k] += 16
        inst.then_inc(self.sems[k], 16)
        self._mark(k, self.val[k], reads, writes)
        self.n_inst += 1
        return inst

    def coll(self, kind, ins, outs, groups, reads=(), writes=()):
        k = self.cc_keys[self.cc_next % len(self.cc_keys)]
        self.cc_next += 1
        deps = self._collect("pool", reads, writes)
        self._need("pool", deps, k, self.val[k])
        self._emit_waits("pool", deps)
        inst = self.eng["pool"].collective_compute(kind, ALU.bypass, replica_groups=groups, ins=ins, outs=outs)
        self.val[k] += 1
        inst.then_inc(self.sems[k], 1)
        self._mark(k, self.val[k], reads, writes)
        return inst

    def wait_all(self, e, tiles):
        deps = {}
        for t in tiles:
            if t.w is not None:
                self._need(e, deps, t.w[0], t.w[1])
            for k, v in t.r.items():
                self._need(e, deps, k, v)
        self._emit_waits(e, deps)

    def barrier(self):
        for e in self.eng:
            deps = {}
            for k, v in self.val.items():
                if k != e:
                    self._need(e, deps, k, v)
            self._emit_waits(e, deps)


def bmid(ap, n):
    a = [list(x) for x in ap.ap]
    return bass.AP(ap.tensor, ap.offset, [a[0], [0, n]] + a[1:])


class Rot:
    def __init__(self, items):
        self.items = items
        self.i = 0

    def get(self):
        it = self.items[self.i % len(self.items)]
        self.i += 1
        return it


def _bf(a):
    return np.ascontiguousarray(a.astype(ml_dtypes.bfloat16))


def const_tables():
    n = np.arange(128)
    ang = 2 * np.pi * np.outer(n, n) / 128.0
    s128 = 1.0 / math.sqrt(128.0)
    c128 = np.cos(ang) * s128
    sn128 = np.sin(ang) * s128
    dft = np.stack([c128, sn128, -sn128, c128], axis=1)
    tw_ang = 2 * np.pi * np.outer(n, n) / float(SEQ)
    tw = np.stack([np.cos(tw_ang), np.sin(tw_ang)], axis=1).astype(np.float32)
    l = np.arange(256)
    a256 = 2 * np.pi * np.outer(l, l) / 256.0
    d256 = np.stack([np.cos(a256) / 16.0, np.sin(a256) / 16.0], axis=1)
    d256 = d256.reshape(2, 128, 2, 256).transpose(1, 0, 2, 3)
    ident = np.eye(128)
    ones = np.ones((128, 128))
    return dict(dft128=_bf(dft), tw=tw, d256=_bf(d256), ident=_bf(ident), ones=_bf(ones))


def band_tables():
    wins = (2, 4, 8, 16)
    band = np.zeros((3, 4, 3, 128, 128), np.float32)
    inv = np.zeros((3, 4, 128), np.float32)
    for v in range(3):
        for g, w in enumerate(wins):
            for t in range(128):
                lo = t - w // 2
                hi = lo + w
                if v == 1:
                    lo = max(lo, 0)
                if v == 2:
                    hi = min(hi, 128)
                cnt = hi - lo
                inv[v, g, t] = 1.0 / cnt
                for tp in range(lo, hi):
                    rel = 1
                    tpp = tp
                    if tp < 0:
                        rel, tpp = 0, tp + 128
                    elif tp >= 128:
                        rel, tpp = 2, tp - 128
                    band[v, g, rel, tpp, t] += 1.0
                band[v, g, 1, t, t] -= cnt
    return band, inv


class Prog:
    def __init__(self, debug=None, layers=L_DEPTH):
        self.debug = debug
        self.layers = layers
        self.nc = bass.Bass("TRN2", target_bir_lowering=False)
        self.inputs = {}
        self.outputs = {}

    def din(self, name, shape, dt=F32):
        t = self.nc.dram_tensor(name, list(shape), dt, kind="ExternalInput")
        self.inputs[name] = t
        return t.ap()

    def dout(self, name, shape, dt=F32):
        t = self.nc.dram_tensor(name, list(shape), dt, kind="ExternalOutput")
        self.outputs[name] = t
        return t.ap()

    def dscr(self, name, shape, dt=BF16):
        if self.debug and (self.debug is True or name in self.debug):
            t = self.nc.dram_tensor(name, list(shape), dt, kind="ExternalOutput")
            self.outputs[name] = t
            return t.ap()
        return self.nc.dram_tensor(name, list(shape), dt).ap()

    def sb(self, st, name, shape, dt):
        self._uid = getattr(self, "_uid", 0) + 1
        return st.enter_context(self.nc.sbuf_tensor("%s_u%d" % (name, self._uid), list(shape), dt))

    def declare(self):
        nl = L_DEPTH
        self.xT = self.din("xT", [D, EXT])
        self.ctxT = self.din("ctxT", [D, CTX])
        self.cT = self.din("cT", [128, KC, 2])
        self.wspec = dict(w_mod=(nl * D, 6 * D), w_in=(nl * D, INW), w_gate=(nl * 4 * D, D), w_pa=(nl * 1024, D), w_pb=(nl * 512, D),
                          w_pc=(nl * 512, D), w_pd=(nl * 512, D), w_o=(nl * D, D), w_ffn_gate=(nl * D, DFF), w_ffn_up=(nl * D, DFF),
                          w_ffn_down=(nl * DFF, D))
        self.wshard = {}
        self.W = {}
        self.TW = {}
        for k, (r, c) in self.wspec.items():
            self.wshard[k] = self.din(k, [r // NCORE, c])
            self.TW[k] = T("W" + k)
        self.b_modT = self.din("b_modT", [nl, 128, 96])
        self.norm1T = self.din("norm1T", [nl, 128, KC])
        self.norm2T = self.din("norm2T", [nl, 128, KC])
        self.qkn = self.din("qkn", [nl, 128, 2])
        self.bias_tab = self.din("bias_tab", [nl, 14, 128, NH, 64])
        self.bias_edge = self.din("bias_edge", [nl, 7, 6, 128, NH, 64])
        self.w_pool = self.din("w_pool", [nl, 4, 128, 128])
        self.pool_scaleT = self.din("pool_scaleT", [nl, 128, 4])
        self.conv_wT = self.din("conv_wT", [nl, 128, 4, 3])
        self.c_dft = self.din("dft128", [128, 4, 128], BF16)
        self.c_tw = self.din("tw", [128, 2, 128])
        self.c_d256 = self.din("d256", [128, 2, 2, 256], BF16)
        self.c_ident = self.din("ident", [128, 128], BF16)
        self.c_ones = self.din("ones", [128, 128], BF16)
        self.c_band = self.din("band", [3, 128, 4, 3, 128], BF16)
        self.c_inv = self.din("invcnt", [3, 128, 4, 128])
        self.c_bandc = self.din("bandc", [2, 128, 4, 3, 128], BF16)
        self.c_invc = self.din("invcntc", [2, 128, 4, 128])
        self.c_edge = self.din("edge", [128, 2])
        self.c_idx = self.din("gidx", [128, 40], I32)
        self.outT = self.dout("outT", [D, OWN])

    def setup(self, st):
        nc = self.nc
        self.s = Sched(nc, st)
        s = self.s
        self.psum_banks = [(st.enter_context(nc.psum_tensor("ps%d" % i, [128, 512], F32)), T("ps%d" % i)) for i in range(8)]
        self.psum = Rot(self.psum_banks)
        self.ident = self.sb(st, "ident", [128, 128], BF16)
        self.ones = self.sb(st, "ones", [128, 128], BF16)
        self.Tconst = T("const")
        s.dma("sp", self.ident[:], self.c_ident[:, :], writes=[self.Tconst])
        s.dma("sp", self.ones[:], self.c_ones[:, :], writes=[self.Tconst])
        self.edge = self.sb(st, "edge", [128, 2], F32)
        s.dma("sp", self.edge[:], self.c_edge[:, :], writes=[self.Tconst])
        self.gidx = self.sb(st, "gidx", [128, 40], I32)
        s.dma("sp", self.gidx[:], self.c_idx[:, :], writes=[self.Tconst])
        self.mod = [self.sb(st, "mod%d" % l, [128, 2, 6, KC], F32) for l in range(L_DEPTH)]
        self.Tmod = [T("mod%d" % l) for l in range(L_DEPTH)]
        self.modA = [self.sb(st, "modA%d" % l, [128, 2, 2, KC], F32) for l in range(L_DEPTH)]
        self.small = [self.sb(st, "small%d" % l, [128, 64], F32) for l in range(L_DEPTH)]

    def fence(self, reads, writes):
        self._fid = getattr(self, "_fid", 0) + 1
        fi = self.nc.dram_tensor("fence_in%d" % self._fid, [16, 64], F32).ap()
        fo = self.nc.dram_tensor("fence_out%d" % self._fid, [16 * NCORE, 64], F32).ap()
        self.s.coll("AllGather", [fi[:, :]], [fo[:, :]], [list(range(NCORE))], reads=reads, writes=writes)

    def gather_weights(self):
        MAXB = 8 * 1024 * 1024
        self.Wslabs = {}
        tws = []
        for k, (r, c) in self.wspec.items():
            rs = r // NCORE
            nblk = max(1, c // 512)
            per = max(1, MAXB // (rs * 512 * 4))
            slabs = []
            b0 = 0
            tw = T("Wraw" + k)
            while b0 < nblk:
                nb = min(per, nblk - b0)
                c0, c1 = b0 * 512, min(c, (b0 + nb) * 512)
                cw = c1 - c0
                bounce = self.nc.dram_tensor("bnc_%s_%d" % (k, b0), [rs, cw], F32).ap()
                full = self.nc.dram_tensor("full_%s_%d" % (k, b0), [r, cw], F32, addr_space="Shared").ap()
                tb = T()
                for r0 in range(0, rs, 128):
                    self.s.dma("sp", bounce[r0:r0 + 128, :], self.wshard[k][r0:r0 + 128, c0:c1], writes=[tb])
                self.s.coll("AllGather", [bounce[:, :]], [full[:, :]], [list(range(NCORE))], reads=[tb], writes=[tw])
                slabs.append((c0, c1, full))
                b0 += nb
            self.Wslabs[k] = slabs
            tws.append(tw)
        self.fence(tws, [self.TW[k] for k in self.wspec])

    def wsrc(self, wname, r0, nrows, c0, ncols):
        for (a, b, full) in self.Wslabs[wname]:
            if a <= c0 and c0 + ncols <= b:
                return full[r0:r0 + nrows, c0 - a:c0 - a + ncols]
        raise ValueError((wname, c0, ncols))

    def wload(self, wname, r0, kchunks, c0, ncols):
        return self._wload(self.wsrc(wname, r0, kchunks * 128, c0, ncols), kchunks, ncols, self.TW[wname])

    def alloc_wbufs(self, st, n=4):
        self.wbufs = Rot([(self.sb(st, "wb%d" % i, [128, 8192], BF16), T("wb%d" % i)) for i in range(n)])

    def _wload(self, src2d, kchunks, ncols, tsrc):
        buf, tb = self.wbufs.get()
        view = buf[:, 0:kchunks * ncols].rearrange("p (k n) -> p k n", n=ncols)
        self.s.dma("pool", view, src2d.rearrange("(k p) n -> p k n", p=128), reads=[tsrc], writes=[tb])
        return view, tb

    def phase_mod(self, st0):
        s = self.s
        with contextlib.ExitStack() as st:
            self.alloc_wbufs(st)
            cT = self.sb(st, "cT", [128, KC, 2], F32)
            sg = self.sb(st, "sg", [128, KC, 2], F32)
            cb = self.sb(st, "cb", [128, KC, 2], BF16)
            bm = self.sb(st, "bm", [128, 96], F32)
            nw = self.sb(st, "nw", [128, 2, KC], F32)
            Tc, Tb = T(), T()
            s.dma("sp", cT[:], self.cT[:, :, :], writes=[Tc])
            s.op("act", lambda e: e.activation(out=sg[:], in_=cT[:], func=AF.Sigmoid), reads=[Tc], writes=[Tb])
            s.op("dve", lambda e: e.tensor_tensor(out=cb[:], in0=cT[:], in1=sg[:], op=ALU.mult), reads=[Tc, Tb], writes=[Tb])
            Tl = T()
            for l in range(self.layers):
                s.dma("sp", bm[:], self.b_modT[l, :, :], writes=[Tl])
                s.dma("sp", nw[:, 0, :], self.norm1T[l, :, :], writes=[Tl])
                s.dma("sp", nw[:, 1, :], self.norm2T[l, :, :], writes=[Tl])
                s.dma("sp", self.small[l][:, 0:2], self.qkn[l, :, :], writes=[self.Tmod[l]])
                s.dma("sp", self.small[l][:, 2:6], self.pool_scaleT[l, :, :], writes=[self.Tmod[l]])
                s.dma("sp", self.small[l][:, 8:20].rearrange("p (a b) -> p a b", b=3), self.conv_wT[l, :, :, :], writes=[self.Tmod[l]])
                for blk in range(24):
                    wv, tw = self.wload("w_mod", l * D, KC, blk * 512, 512)
                    ps, tp = self.psum.get()
                    for c in range(4):
                        for k in range(KC):
                            s.op("pe", lambda e, c=c, k=k: e.matmul(ps[:, c * 2:c * 2 + 2], lhsT=wv[:, k, c * 128:(c + 1) * 128], rhs=cb[:, k, :],
                                                                    start=(k == 0), stop=(k == KC - 1)),
                                 reads=[tw, Tb], writes=[tp], pe_accum=True)
                    kk = (blk * 4) // 16
                    kc0 = (blk * 4) % 16
                    for r in range(2):
                        s.op("dve", lambda e, r=r, kk=kk, kc0=kc0, blk=blk: e.tensor_tensor(
                            out=self.mod[l][:, r, kk, kc0:kc0 + 4], in0=ps[:, r:8:2], in1=bm[:, blk * 4:blk * 4 + 4], op=ALU.add),
                            reads=[tp, Tl], writes=[self.Tmod[l]])
                for r in range(2):
                    for j, kk in enumerate((1, 4)):
                        s.op("dve", lambda e, r=r, j=j, kk=kk: e.scalar_tensor_tensor(
                            out=self.modA[l][:, r, j, :], in0=self.mod[l][:, r, kk, :], scalar=1.0, in1=nw[:, j, :], op0=ALU.add, op1=ALU.mult),
                            reads=[self.Tmod[l], Tl], writes=[self.Tmod[l]])
            s.barrier()

    def declare_scratch(self):
        self.scr = {}
        for kind, E in (("lat", EXT), ("ctx", CTX)):
            d = {}
            d["QT"] = self.dscr(kind + "_QT", [NH, 128, E])
            d["KT"] = self.dscr(kind + "_KT", [NH, 128, E])
            d["V"] = self.dscr(kind + "_V", [E, 1024])
            d["PU"] = self.dscr(kind + "_PU", [E, 512])
            d["SC3"] = self.dscr(kind + "_SC3", [3, 512, E])
            n_own = OWN if kind == "lat" else CTX
            d["FT"] = self.dscr(kind + "_FT", [4, n_own, 128])
            d["HT"] = self.dscr(kind + "_HT", [D, n_own])
            d["AT"] = self.dscr(kind + "_AT", [1024, n_own])
            d["BT"] = self.dscr(kind + "_BT", [512, n_own])
            d["CT"] = self.dscr(kind + "_CT", [512, n_own])
            d["DT"] = self.dscr(kind + "_DT", [512, n_own])
            d["T"] = {k: T(kind + k) for k in ("QT", "KT", "V", "PU", "SC3", "FT", "HT", "AT", "BT", "CT", "DT")}
            self.scr[kind] = d
        self.x1T = self.dscr("x1T", [D, EXT], F32)
        self.Tx1 = T("x1T")
        self.xcT = self.dscr("xcT", [D, CTX], F32)
        self.Txc = T("xcT")
        self.Txin = T("xin")
        self.Tout = T("out")
        self.Tx1n = T("x1n")

    def phase1(self, l, kind, only_kv=False):
        s = self.s
        sc = self.scr[kind]
        TS = sc["T"]
        r = 0 if kind == "lat" else 1
        if kind == "lat":
            src = self.xT if l == 0 else self.x1T
            Tsrc = self.Txin if l == 0 else self.Tx1
            tiles = [dict(cols=[(0, 0, HALO), (HALO, HALO + OWN, HALO)], own=None)]
            tiles += [dict(cols=[(0, HALO + i * TT, TT)], own=i * TT) for i in range(NTILE)]
            N = TT
        else:
            src = self.ctxT if l == 0 else self.xcT
            Tsrc = self.Txin if l == 0 else self.Txc
            tiles = [dict(cols=[(0, 0, CTX)], own=0)]
            N = CTX
        NS = N // 128
        with contextlib.ExitStack() as st:
            self.alloc_wbufs(st)
            xt = self.sb(st, "p1_xt", [128, KC, N], F32)
            sq = self.sb(st, "p1_sq", [128, KC, N], BF16)
            hT = self.sb(st, "p1_hT", [128, KC, N], BF16)
            lnv = self.sb(st, "p1_ln", [128, N], F32)
            rstd = self.sb(st, "p1_rstd", [128, N], F32)
            stq = Rot([(self.sb(st, "p1_stq%d" % i, [128, NH, N], BF16), T()) for i in range(2)])
            st3 = Rot([(self.sb(st, "p1_st3%d" % i, [128, 4, N], BF16), T()) for i in range(2)])
            stt = Rot([(self.sb(st, "p1_stt%d" % i, [128, NS, 512], BF16), T()) for i in range(2)])
            sqh = Rot([(self.sb(st, "p1_sqh%d" % i, [128, N], BF16), T()) for i in range(2)])
            lnh = Rot([(self.sb(st, "p1_lnh%d" % i, [128, N], F32), T()) for i in range(2)])
            Txt, Tsq, ThT, Tln, Trs = T(), T(), T(), T(), T()
            A1 = self.modA[l]
            M = self.mod[l]
            sm = self.small[l]
            ev = [0]

            def evac(out, in_, reads, writes):
                ev[0] += 1
                if ev[0] % 2:
                    s.op("act", lambda e: e.activation(out=out, in_=in_, func=AF.Copy), reads=reads, writes=writes)
                else:
                    s.op("dve", lambda e: e.tensor_copy(out=out, in_=in_), reads=reads, writes=writes)

            for tl in tiles:
                own = tl["own"]
                for (to, es, n) in tl["cols"]:
                    s.dma("sp", xt[:, :, to:to + n], src[:, es:es + n].rearrange("(k p) t -> p k t", p=128), reads=[Tsrc], writes=[Txt])
                s.op("act", lambda e: e.activation(out=sq[:], in_=xt[:], func=AF.Square), reads=[Txt], writes=[Tsq])
                ps, tp = self.psum.get()
                for k in range(KC):
                    s.op("pe", lambda e, k=k: e.matmul(ps[:, 0:N], lhsT=self.ones[:], rhs=sq[:, k, :], start=(k == 0), stop=(k == KC - 1)),
                         reads=[Tsq, self.Tconst], writes=[tp], pe_accum=True)
                s.op("act", lambda e: e.activation(out=lnv[:], in_=ps[:, 0:N], func=AF.Ln, scale=1.0 / D, bias=EPS), reads=[tp], writes=[Tln])
                s.op("act", lambda e: e.activation(out=rstd[:], in_=lnv[:], func=AF.Exp, scale=-0.5), reads=[Tln], writes=[Trs])
                rb = bmid(rstd[:], KC)
                s.op("dve", lambda e: e.tensor_tensor(out=xt[:], in0=xt[:], in1=rb, op=ALU.mult), reads=[Txt, Trs], writes=[Txt])
                for k in range(KC):
                    if k % 2 == 0:
                        s.op("act", lambda e, k=k: e.activation(out=hT[:, k, :], in_=xt[:, k, :], func=AF.Identity,
                                                                scale=A1[:, r, 0, k:k + 1], bias=M[:, r, 0, k:k + 1]),
                             reads=[Txt, self.Tmod[l]], writes=[ThT])
                    else:
                        s.op("dve", lambda e, k=k: e.tensor_scalar(out=hT[:, k, :], in0=xt[:, k, :], scalar1=A1[:, r, 0, k:k + 1],
                                                                   scalar2=M[:, r, 0, k:k + 1], op0=ALU.mult, op1=ALU.add),
                             reads=[Txt, self.Tmod[l]], writes=[ThT])
                if own is not None and not only_kv:
                    s.dma("act", sc["HT"][:, own:own + N].rearrange("(k p) t -> p k t", p=128), hT[:], reads=[ThT], writes=[TS["HT"]])
                for j in range(11):
                    if only_kv and j not in (2, 3, 4, 5):
                        continue
                    if own is None and j in (0, 1, 6):
                        continue
                    wv, tw = self.wload("w_in", l * D, KC, j * 512, 512)
                    if j in (0, 1, 2, 3, 8, 9, 10):
                        if j in (0, 2):
                            stg, tstg = stq.get()
                        if j >= 8:
                            stg, tstg = st3.get()
                        for c in range(4):
                            ps, tp = self.psum.get()
                            for k in range(KC):
                                s.op("pe", lambda e, k=k, c=c: e.matmul(ps[:, 0:N], lhsT=wv[:, k, c * 128:(c + 1) * 128], rhs=hT[:, k, :],
                                                                        start=(k == 0), stop=(k == KC - 1)),
                                     reads=[tw, ThT], writes=[tp], pe_accum=True)
                            if j < 4:
                                h = (j % 2) * 4 + c
                                q_, tq_ = sqh.get()
                                l_, tl_ = lnh.get()
                                s.op("act", lambda e: e.activation(out=q_[:], in_=ps[:, 0:N], func=AF.Square), reads=[tp], writes=[tq_])
                                ps2, tp2 = self.psum.get()
                                s.op("pe", lambda e: e.matmul(ps2[:, 0:N], lhsT=self.ones[:], rhs=q_[:], start=True, stop=True),
                                     reads=[tq_, self.Tconst], writes=[tp2])
                                s.op("act", lambda e: e.activation(out=l_[:], in_=ps2[:, 0:N], func=AF.Ln, scale=1.0 / DH, bias=EPS), reads=[tp2], writes=[tl_])
                                s.op("act", lambda e: e.activation(out=l_[:], in_=l_[:], func=AF.Exp, scale=-0.5), reads=[tl_], writes=[tl_])
                                wcol = sm[:, (j // 2):(j // 2) + 1]
                                s.op("dve", lambda e, h=h: e.scalar_tensor_tensor(out=stg[:, h, :], in0=ps[:, 0:N], scalar=wcol, in1=l_[:],
                                                                                 op0=ALU.mult, op1=ALU.mult),
                                     reads=[tp, tl_, self.Tmod[l]], writes=[tstg])
                            else:
                                evac(stg[:, c, :], ps[:, 0:N], [tp], [tstg])
                        if j in (1, 3):
                            name = "QT" if j == 1 else "KT"
                            for (to, es, n) in tl["cols"]:
                                s.dma("act", sc[name][:, :, es:es + n].rearrange("h p t -> p h t"), stg[:, :, to:to + n], reads=[tstg], writes=[TS[name]])
                        if j >= 8:
                            for (to, es, n) in tl["cols"]:
                                s.dma("act", sc["SC3"][j - 8, :, es:es + n].rearrange("(c p) t -> p c t", p=128), stg[:, :, to:to + n],
                                      reads=[tstg], writes=[TS["SC3"]])
                    else:
                        stg, tstg = stt.get()
                        for sub in range(NS):
                            ps, tp = self.psum.get()
                            for k in range(KC):
                                s.op("pe", lambda e, k=k, sub=sub: e.matmul(ps[:, :], lhsT=hT[:, k, sub * 128:(sub + 1) * 128], rhs=wv[:, k, :],
                                                                            start=(k == 0), stop=(k == KC - 1)),
                                     reads=[tw, ThT], writes=[tp], pe_accum=True)
                            evac(stg[:, sub, :], ps[:, :], [tp], [tstg])
                        for (to, es, n) in tl["cols"]:
                            s0, ns = to // 128, n // 128
                            if j in (4, 5):
                                s.dma("act", sc["V"][es:es + n, (j - 4) * 512:(j - 3) * 512].rearrange("(s p) c -> p s c", p=128),
                                      stg[:, s0:s0 + ns, :], reads=[tstg], writes=[TS["V"]])
                            elif j == 7:
                                s.dma("act", sc["PU"][es:es + n, :].rearrange("(s p) c -> p s c", p=128), stg[:, s0:s0 + ns, :],
                                      reads=[tstg], writes=[TS["PU"]])
                            else:
                                for g in range(4):
                                    s.dma("act", sc["FT"][g, own:own + n, :].rearrange("(s p) c -> p s c", p=128),
                                          stg[:, s0:s0 + ns, g * 128:(g + 1) * 128], reads=[tstg], writes=[TS["FT"]])
            s.barrier()

    def dump(self, name, sb_ap, shape, tile, dt=F32):
        o = self.dout(name, shape, dt)
        self.s.dma("sp", o, sb_ap, reads=[tile], writes=[T()])


def prep_inputs(inp):
    f32 = np.float32
    x = np.asarray(inp["x"], f32)
    ctx = np.asarray(inp["ctx"], f32)
    c = np.asarray(inp["c"], f32)
    c_ctx = np.asarray(inp["c_ctx"], f32)
    nl = L_DEPTH
    consts = const_tables()
    band, inv = band_tables()
    band_l = np.transpose(band, (0, 3, 1, 2, 4))
    inv_rep = np.broadcast_to(inv[:, None, :, :], (3, 128, 4, 128))
    shared = {}
    wnames = ["w_mod", "w_in", "w_gate", "w_pa", "w_pb", "w_pc", "w_pd", "w_o", "w_ffn_gate", "w_ffn_up", "w_ffn_down"]
    wflat = {k: np.asarray(inp[k], f32).reshape(-1, inp[k].shape[-1]) for k in wnames}

    def pk(v, n):
        v = np.asarray(v, f32)
        return np.ascontiguousarray(np.swapaxes(v.reshape(v.shape[:-1] + (n, 128)), -1, -2))

    shared["b_modT"] = pk(inp["b_mod"], 96)
    shared["norm1T"] = pk(inp["norm1_w"], KC)
    shared["norm2T"] = pk(inp["norm2_w"], KC)
    shared["qkn"] = np.ascontiguousarray(np.stack([np.asarray(inp["q_norm_w"], f32), np.asarray(inp["k_norm_w"], f32)], axis=-1))
    rpb = np.asarray(inp["rpb"], f32)
    cols = np.arange(64)
    cstart = np.clip(cols - 8, 0, 48)
    bt = np.full((nl, 14, 128, NH, 64), NEG, f32)
    kc = np.arange(64)
    valid = (kc[:, None] >= cstart[None, :]) & (kc[:, None] < cstart[None, :] + 16)
    off = np.clip(kc[:, None] - cols[None, :] + 15, 0, 30)
    for d0 in range(14):
        for half in range(2):
            dr = d0 - 7 + half
            if dr < -7 or dr > 7:
                continue
            g = rpb[:, :, dr + 7, :][:, :, off]
            g = np.where(valid[None, None], g, NEG)
            bt[:, d0, half * 64:(half + 1) * 64, :, :] = np.transpose(g, (0, 2, 1, 3))
    shared["bias_tab"] = bt
    shared["w_pool"] = np.asarray(inp["w_pool"], f32)
    shared["pool_scaleT"] = pk(inp["pool_scale"], 4)
    cw = np.asarray(inp["conv_w"], f32)
    shared["conv_wT"] = np.ascontiguousarray(np.transpose(cw.reshape(nl, 3, 4, 128), (0, 3, 2, 1)))
    shared["dft128"] = consts["dft128"]
    shared["tw"] = consts["tw"]
    shared["d256"] = consts["d256"]
    shared["ident"] = consts["ident"]
    shared["ones"] = consts["ones"]
    shared["bandc"] = _bf(band_l[[1, 2]])
    shared["invcntc"] = np.ascontiguousarray(inv_rep[[1, 2]])
    maps = []
    for core in range(NCORE):
        b, j = core // 4, core % 4
        m = dict(shared)
        xe = np.zeros((EXT, D), f32)
        lo = j * OWN - HALO
        hi = j * OWN + OWN + HALO
        slo, shi = max(lo, 0), min(hi, SEQ)
        xe[slo - lo:shi - lo] = x[b, slo:shi]
        m["xT"] = np.ascontiguousarray(xe.T)
        m["ctxT"] = np.ascontiguousarray(ctx[b].T)
        m["cT"] = np.ascontiguousarray(np.stack([pk(c[b], KC), pk(c_ctx, KC)], axis=-1))
        for k in wnames:
            r = wflat[k].shape[0] // NCORE
            m[k] = wflat[k][core * r:(core + 1) * r]
        first_v = 1 if j == 0 else 0
        last_v = 2 if j == 3 else 0
        m["band"] = _bf(band_l[[first_v, 0, last_v]])
        m["invcnt"] = np.ascontiguousarray(inv_rep[[first_v, 0, last_v]])
        m["edge"] = np.broadcast_to(np.array([[0.0 if j == 0 else 1.0, 0.0 if j == 3 else 1.0]], f32), (128, 2)).copy()
        gi = np.zeros((128, 40), np.int32)
        p = np.arange(128)
        BIG = 1 << 30
        gi[:, 0] = ((4 * b + p // 32) * 4 + j) * 32 + (p % 32)
        for g in range(4):
            gi[:, 1 + g] = ((4 * b + g) * 128 + p) * 4 + j
        for k in range(KC):
            gi[:, 8 + k] = ((4 * b + max(j - 1, 0)) * 2 + 1) * D + k * 128 + p
            gi[:, 24 + k] = ((4 * b + min(j + 1, 3)) * 2 + 0) * D + k * 128 + p
        m["gidx"] = gi
        be = np.full((nl, 7, 6, 128, NH, 64), NEG, f32)
        for si, rl in enumerate([0, 1, 2, 3, 61, 62, 63]):
            eb = 0 if si < 4 else 60
            gr = 64 * j + rl
            rs = min(max(gr - 4, 0), 248)
            for cch in range(6):
                for hf in range(2):
                    gk = 64 * j + eb + 2 * cch + hf - 4
                    if gk < rs or gk >= rs + 8:
                        continue
                    dr = gk - gr
                    g = rpb[:, :, dr + 7, :][:, :, off]
                    g = np.where(valid[None, None], g, NEG)
                    be[:, si, cch, hf * 64:(hf + 1) * 64, :, :] = np.transpose(g, (0, 2, 1, 3))
        m["bias_edge"] = be
        maps.append(m)
    return maps


def build(stop=None, debug=False):
    p = Prog(debug=debug)
    p.declare()
    p.declare_scratch()
    st = contextlib.ExitStack()
    with st:
        p.setup(st)
        p.gather_weights()
        p.phase_mod(st)
        if debug:
            for l in range(L_DEPTH):
                p.dump("dbg_mod%d" % l, p.mod[l][:], [128, 2, 6, KC], p.Tmod[l])
                p.dump("dbg_modA%d" % l, p.modA[l][:], [128, 2, 2, KC], p.Tmod[l])
        for l in range(L_DEPTH):
            if stop == "mod":
                break
            p.phase1(l, "ctx", only_kv=(l == L_DEPTH - 1))
            if l < L_DEPTH - 1:
                p.phase_attn_ctx(l)
                p.phase_dft_ctx(l)
                p.phase_poolconv(l, "ctx")
                p.phase3(l, "ctx")
            if stop == "ctx0":
                break
            p.phase1(l, "lat")
            if stop == "p1":
                break
            p.phase_fft_lat(l)
            if stop == "fft0":
                break
            p.phase_attn_lat(l)
            p.phase_poolconv(l, "lat")
            if stop == "mix0":
                break
            p.phase3(l, "lat")
            if l < L_DEPTH - 1:
                p.phase_halo()
            if stop == "l0":
                break
        p.s.barrier()
    return p


def run(p, maps, trace=False):
    names = set(p.inputs.keys())
    in_maps = [{k: v for k, v in m.items() if k in names} for m in maps]
    for k in names:
        assert k in in_maps[0], k
    return run_bass_kernel_spmd(p.nc, in_maps, core_ids=list(range(NCORE)), trace=trace)


def _attention_core(self, l, st, qT, tq, nq, chunks, hsl, pt_pool, out_o, out_s, col0, E_reads, psrot=None):
    s = self.s
    nch = len(chunks)
    ps, tp = (psrot or self.psum).get()
    for ci, (kT, vv, ee, rd) in enumerate(chunks):
        s.op("pe", lambda e, ci=ci, kT=kT: e.matmul(ps[:, ci * nq:(ci + 1) * nq], lhsT=kT, rhs=qT, start=True, stop=True),
             reads=[tq] + rd, writes=[tp], pe_accum=True)
    pt, tpt = pt_pool.get()
    s.op("act", lambda e: e.activation(out=pt[:, 0:nch * nq], in_=ps[:, 0:nch * nq], func=AF.Exp, scale=DH ** -0.5), reads=[tp], writes=[tpt])
    return pt, tpt


def phase_attn_lat(self, l):
    s = self.s
    sc = self.scr["lat"]
    cc = self.scr["ctx"]
    TS = sc["T"]
    with contextlib.ExitStack() as st:
        Eb = self.sb(st, "at_Eb", [128, 14, NH, 64], BF16)
        Ee = self.sb(st, "at_Ee", [128, 7, 6, NH, 64], BF16)
        kcT = self.sb(st, "at_kcT", [128, NH, CTX], BF16)
        vc = self.sb(st, "at_vc", [128, 2, 1024], BF16)
        TE, Tkc = T(), T()
        st_tmp = contextlib.ExitStack()
        tmpE = self.sb(st_tmp, "at_tmpE", [128, 14 * NH * 64], F32)
        s.dma("sp", tmpE[:].rearrange("p (a x) -> p a x", a=14), self.bias_tab[l].rearrange("a p h c -> p a (h c)"), writes=[TE])
        s.op("act", lambda e: e.activation(out=Eb[:].rearrange("p a h c -> p (a h c)"), in_=tmpE[:], func=AF.Exp), reads=[TE], writes=[TE])
        for si in range(7):
            ne = 6 * NH * 64
            s.dma("sp", tmpE[:, 0:ne].rearrange("p (c x) -> p c x", c=6), self.bias_edge[l, si].rearrange("c p h x -> p c (h x)"), reads=[TE], writes=[TE])
            s.op("act", lambda e, si=si: e.activation(out=Ee[:, si].rearrange("p c h x -> p (c h x)"), in_=tmpE[:, 0:ne], func=AF.Exp),
                 reads=[TE], writes=[TE])
        s.dma("sp", kcT[:], cc["KT"].rearrange("h p t -> p h t"), reads=[cc["T"]["KT"]], writes=[Tkc])
        s.dma("sp", vc[:], cc["V"].rearrange("(s p) c -> p s c", p=128), reads=[cc["T"]["V"]], writes=[Tkc])
        s.barrier()
        st_tmp.close()
        qb = Rot([(self.sb(st, "at_q%d" % i, [128, NH, TT], BF16), T()) for i in range(2)])
        kb = Rot([(self.sb(st, "at_k%d" % i, [128, NH, 1024], BF16), T()) for i in range(1)])
        ve = Rot([(self.sb(st, "at_ve%d" % i, [128, 8, 1024], BF16), T()) for i in range(1)])
        vo = Rot([(self.sb(st, "at_vo%d" % i, [128, 7, 1024], BF16), T()) for i in range(1)])
        ob = Rot([(self.sb(st, "at_o%d" % i, [128, NH, TT], BF16), T()) for i in range(2)])
        ptp = Rot([(self.sb(st, "at_pt%d" % i, [128, 512], BF16), T()) for i in range(3)])
        rec = self.sb(st, "at_rec", [128, 512], F32)
        Trec = T()
        acc_rot = Rot(self.psum_banks[0:4])
        s_rot = Rot(self.psum_banks[4:8])
        for i in range(NTILE):
            qT, tq = qb.get()
            kT, tk = kb.get()
            vE, tve = ve.get()
            vO, tvo = vo.get()
            oT, to = ob.get()
            e0 = i * TT
            s.dma("sp", qT[:], sc["QT"][:, :, HALO + i * TT:HALO + (i + 1) * TT].rearrange("h p t -> p h t"), reads=[TS["QT"]], writes=[tq])
            s.dma("sp", kT[:], sc["KT"][:, :, e0:e0 + 1024].rearrange("h p t -> p h t"), reads=[TS["KT"]], writes=[tk])
            s.dma("sp", vE[:], sc["V"][e0:e0 + 1024, :].rearrange("(s p) c -> p s c", p=128), reads=[TS["V"]], writes=[tve])
            s.dma("sp", vO[:], sc["V"][e0 + 64:e0 + 64 + 896, :].rearrange("(s p) c -> p s c", p=128), reads=[TS["V"]], writes=[tvo])
            for rr in range(8):
                r = 8 * i + rr
                if r < 4:
                    loc = [(2 * c, ("e", r, c)) for c in range(6)]
                elif r > 60:
                    loc = [(4 + 2 * c, ("e", 4 + (r - 61), c)) for c in range(5)]
                else:
                    loc = [(rr + 2 * c, ("b", 3 + 2 * c)) for c in range(4)]
                pso, tpo = acc_rot.get()
                pss, tps = acc_rot.get()
                for h in range(NH):
                    chunks = []
                    for (brow, _) in loc:
                        chunks.append((kT[:, h, brow * 64:brow * 64 + 128], None, None, [tk]))
                    for cx in range(2):
                        chunks.append((kcT[:, h, cx * 128:(cx + 1) * 128], None, None, [Tkc]))
                    nl = len(loc)
                    pt, tpt = _attention_core(self, l, st, qT[:, h, rr * 64:(rr + 1) * 64], tq, 64, chunks, h, ptp, None, None, 0, None, psrot=s_rot)
                    if loc[0][1][0] == "b":
                        e_ap = Eb[:, 3:11:2, h, :]
                    else:
                        e_ap = Ee[:, loc[0][1][1], 0:nl, h, :]
                    s.op("dve", lambda e: e.tensor_tensor(out=pt[:, 0:nl * 64].rearrange("p (c x) -> p c x", x=64),
                                                          in0=pt[:, 0:nl * 64].rearrange("p (c x) -> p c x", x=64), in1=e_ap, op=ALU.mult),
                         reads=[tpt, TE], writes=[tpt])
                    nch = nl + 2
                    for ci in range(nch):
                        if ci < nl:
                            brow = loc[ci][0]
                            if brow % 2 == 0:
                                vv, tv = vE[:, brow // 2, h * 128:(h + 1) * 128], tve
                            else:
                                vv, tv = vO[:, (brow - 1) // 2, h * 128:(h + 1) * 128], tvo
                        else:
                            vv, tv = vc[:, ci - nl, h * 128:(h + 1) * 128], Tkc
                        s.op("pe", lambda e, ci=ci, vv=vv: e.matmul(pso[:, h * 64:(h + 1) * 64], lhsT=vv, rhs=pt[:, ci * 64:(ci + 1) * 64],
                                                                    start=(ci == 0), stop=(ci == nch - 1)),
                             reads=[tpt, tv], writes=[tpo], pe_accum=True)
                    for ci in range(nch):
                        s.op("pe", lambda e, ci=ci: e.matmul(pss[:, h * 64:(h + 1) * 64], lhsT=self.ones[:], rhs=pt[:, ci * 64:(ci + 1) * 64],
                                                             start=(ci == 0), stop=(ci == nch - 1)),
                             reads=[tpt, self.Tconst], writes=[tps], pe_accum=True)
                s.op("dve", lambda e: e.reciprocal(out=rec[:], in_=pss[:, :]), reads=[tps], writes=[Trec])
                s.op("dve", lambda e: e.tensor_tensor(out=oT[:, :, rr * 64:(rr + 1) * 64], in0=pso[:, :].rearrange("p (h x) -> p h x", x=64),
                                                      in1=rec[:].rearrange("p (h x) -> p h x", x=64), op=ALU.mult),
                     reads=[tpo, Trec], writes=[to])
            s.dma("act", sc["AT"][:, i * TT:(i + 1) * TT].rearrange("(h p) t -> p h t", p=128), oT[:], reads=[to], writes=[TS["AT"]])
        s.barrier()


def phase_attn_ctx(self, l):
    s = self.s
    cc = self.scr["ctx"]
    TS = cc["T"]
    with contextlib.ExitStack() as st:
        qcT = self.sb(st, "ac_q", [128, NH, CTX], BF16)
        kcT = self.sb(st, "ac_k", [128, NH, CTX], BF16)
        vc = self.sb(st, "ac_v", [128, 2, 1024], BF16)
        oT = self.sb(st, "ac_o", [128, NH, CTX], BF16)
        rec = self.sb(st, "ac_rec", [128, 512], F32)
        ptp = Rot([(self.sb(st, "ac_pt%d" % i, [128, 512], BF16), T()) for i in range(2)])
        Tl, To, Trec = T(), T(), T()
        s.dma("sp", qcT[:], cc["QT"].rearrange("h p t -> p h t"), reads=[TS["QT"]], writes=[Tl])
        s.dma("sp", kcT[:], cc["KT"].rearrange("h p t -> p h t"), reads=[TS["KT"]], writes=[Tl])
        s.dma("sp", vc[:], cc["V"].rearrange("(s p) c -> p s c", p=128), reads=[TS["V"]], writes=[Tl])
        for hp in range(NH // 2):
            pso, tpo = self.psum.get()
            pss, tps = self.psum.get()
            for hh in range(2):
                h = hp * 2 + hh
                chunks = [(kcT[:, h, cx * 128:(cx + 1) * 128], None, None, [Tl]) for cx in range(2)]
                pt, tpt = _attention_core(self, l, st, qcT[:, h, :], Tl, CTX, chunks, h, ptp, None, None, 0, None)
                for ci in range(2):
                    s.op("pe", lambda e, ci=ci: e.matmul(pso[:, hh * CTX:(hh + 1) * CTX], lhsT=vc[:, ci, h * 128:(h + 1) * 128],
                                                         rhs=pt[:, ci * CTX:(ci + 1) * CTX], start=(ci == 0), stop=(ci == 1)),
                         reads=[tpt, Tl], writes=[tpo], pe_accum=True)
                for ci in range(2):
                    s.op("pe", lambda e, ci=ci: e.matmul(pss[:, hh * CTX:(hh + 1) * CTX], lhsT=self.ones[:], rhs=pt[:, ci * CTX:(ci + 1) * CTX],
                                                         start=(ci == 0), stop=(ci == 1)),
                         reads=[tpt, self.Tconst], writes=[tps], pe_accum=True)
            s.op("dve", lambda e: e.reciprocal(out=rec[:], in_=pss[:, :]), reads=[tps], writes=[Trec])
            s.op("dve", lambda e: e.tensor_tensor(out=oT[:, hp * 2:hp * 2 + 2, :], in0=pso[:, :].rearrange("p (h x) -> p h x", x=CTX),
                                                  in1=rec[:].rearrange("p (h x) -> p h x", x=CTX), op=ALU.mult),
                 reads=[tpo, Trec], writes=[To])
        s.dma("act", cc["AT"].rearrange("(h p) t -> p h t", p=128), oT[:], reads=[To], writes=[TS["AT"]])
        s.barrier()


Prog.phase_attn_lat = phase_attn_lat
Prog.phase_attn_ctx = phase_attn_ctx


def phase_poolconv(self, l, kind):
    s = self.s
    sc = self.scr[kind]
    TS = sc["T"]
    lat = kind == "lat"
    N = TT if lat else CTX
    NS = N // 128
    ntile = NTILE if lat else 1
    sm = self.small[l]
    with contextlib.ExitStack() as st:
        band = self.sb(st, "pc_band", [128, 3, 4, 3, 128], BF16)
        inv = self.sb(st, "pc_inv", [128, 3, 4, 128], F32)
        wpl = self.sb(st, "pc_wpl", [128, 4, 128], BF16)
        Tc = T()
        if lat:
            s.dma("sp", band[:], self.c_band.rearrange("v p g r t -> p v g r t"), writes=[Tc])
            s.dma("sp", inv[:], self.c_inv.rearrange("v p g t -> p v g t"), writes=[Tc])
        else:
            s.dma("sp", band[:, 0:2], self.c_bandc.rearrange("v p g r t -> p v g r t"), writes=[Tc])
            s.dma("sp", inv[:, 0:2], self.c_invc.rearrange("v p g t -> p v g t"), writes=[Tc])
        s.dma("pool", wpl[:], self.w_pool[l].rearrange("g c d -> c g d"), writes=[Tc])
        puw = Rot([(self.sb(st, "pc_pu%d" % i, [128, NS + 2, 512], BF16), T()) for i in range(2)])
        pT = Rot([(self.sb(st, "pc_pT%d" % i, [128, 4, N], BF16), T()) for i in range(2)])
        cT = Rot([(self.sb(st, "pc_cT%d" % i, [128, 4, N], BF16), T()) for i in range(2)])
        s3 = Rot([(self.sb(st, "pc_s3%d" % i, [128, 3, 4, N + 2], BF16), T()) for i in range(2)])
        u = self.sb(st, "pc_u", [128, 4, N + 2], F32)
        y = self.sb(st, "pc_y", [128, 4, N], F32)
        dT = Rot([(self.sb(st, "pc_dT%d" % i, [128, 4, N], BF16), T()) for i in range(2)])
        Tu, Ty = T(), T()
        for i in range(ntile):
            pw, tpw = puw.get()
            if lat:
                t0 = HALO + i * TT - 128
                s.dma("sp", pw[:], sc["PU"][t0:t0 + (NS + 2) * 128, :].rearrange("(s p) c -> p s c", p=128), reads=[TS["PU"]], writes=[tpw])
            else:
                s.dma("sp", pw[:, 1:1 + NS, :], sc["PU"][:, :].rearrange("(s p) c -> p s c", p=128), reads=[TS["PU"]], writes=[tpw])
            pt_, tpt_ = pT.get()
            for g in range(4):
                ps, tp = self.psum.get()
                for so in range(NS):
                    if lat:
                        v = 0 if (i == 0 and so == 0) else (2 if (i == ntile - 1 and so == NS - 1) else 1)
                    else:
                        v = so
                    rels = [0, 1, 2]
                    if not lat:
                        rels = [1] + ([2] if so == 0 else [0])
                    for ri, rel in enumerate(rels):
                        s.op("pe", lambda e, so=so, rel=rel, v=v, g=g, ri=ri, nr=len(rels): e.matmul(
                            ps[:, so * 128:(so + 1) * 128], lhsT=pw[:, so + rel, g * 128:(g + 1) * 128], rhs=band[:, v, g, rel, :],
                            start=(ri == 0), stop=(ri == nr - 1)), reads=[tpw, Tc], writes=[tp], pe_accum=True)
                for so in range(NS):
                    if lat:
                        v = 0 if (i == 0 and so == 0) else (2 if (i == ntile - 1 and so == NS - 1) else 1)
                    else:
                        v = so
                    s.op("dve", lambda e, so=so, v=v, g=g: e.tensor_tensor(out=pt_[:, g, so * 128:(so + 1) * 128], in0=ps[:, so * 128:(so + 1) * 128],
                                                                         in1=inv[:, v, g, :], op=ALU.mult), reads=[tp, Tc], writes=[tpt_])
            ct_, tct_ = cT.get()
            for g in range(4):
                ps, tp = self.psum.get()
                s.op("pe", lambda e, g=g: e.matmul(ps[:, 0:N], lhsT=wpl[:, g, :], rhs=pt_[:, g, :], start=True, stop=True), reads=[tpt_, Tc], writes=[tp])
                s.op("act", lambda e, g=g: e.activation(out=ct_[:, g, :], in_=ps[:, 0:N], func=AF.Copy, scale=sm[:, 2 + g:3 + g]),
                     reads=[tp, self.Tmod[l]], writes=[tct_])
            s.dma("act", sc["CT"][:, i * N:(i + 1) * N].rearrange("(g p) t -> p g t", p=128), ct_[:], reads=[tct_], writes=[TS["CT"]])
            sw, tsw = s3.get()
            for kk in range(3):
                if lat:
                    t0 = HALO + i * TT - 1
                    s.dma("sp", sw[:, kk], sc["SC3"][kk, :, t0:t0 + N + 2].rearrange("(c p) t -> p c t", p=128), reads=[TS["SC3"]], writes=[tsw])
                else:
                    s.dma("sp", sw[:, kk, :, 1:N + 1], sc["SC3"][kk, :, :].rearrange("(c p) t -> p c t", p=128), reads=[TS["SC3"]], writes=[tsw])
            if lat:
                s.op("dve", lambda e: e.tensor_tensor(out=u[:], in0=sw[:, 2], in1=sw[:, 0], op=ALU.mult), reads=[tsw], writes=[Tu])
                if i == 0:
                    s.op("dve", lambda e: e.tensor_scalar(out=u[:, :, 0:1], in0=u[:, :, 0:1], scalar1=self.edge[:, 0:1], scalar2=None, op0=ALU.mult),
                         reads=[Tu, self.Tconst], writes=[Tu])
                if i == ntile - 1:
                    s.op("dve", lambda e: e.tensor_scalar(out=u[:, :, N + 1:N + 2], in0=u[:, :, N + 1:N + 2], scalar1=self.edge[:, 1:2], scalar2=None,
                                                          op0=ALU.mult), reads=[Tu, self.Tconst], writes=[Tu])
            else:
                s.op("dve", lambda e: e.memset(u[:], 0.0), writes=[Tu])
                s.op("dve", lambda e: e.tensor_tensor(out=u[:, :, 1:N + 1], in0=sw[:, 2, :, 1:N + 1], in1=sw[:, 0, :, 1:N + 1], op=ALU.mult),
                     reads=[tsw], writes=[Tu])
            dt_, tdt_ = dT.get()
            for c in range(4):
                cw = lambda k, c=c: sm[:, 8 + c * 3 + k:9 + c * 3 + k]
                s.op("dve", lambda e, c=c: e.tensor_scalar(out=y[:, c, :], in0=u[:, c, 1:N + 1], scalar1=cw(1), scalar2=None, op0=ALU.mult),
                     reads=[Tu, self.Tmod[l]], writes=[Ty])
                s.op("dve", lambda e, c=c: e.scalar_tensor_tensor(out=y[:, c, :], in0=u[:, c, 0:N], scalar=cw(0), in1=y[:, c, :], op0=ALU.mult, op1=ALU.add),
                     reads=[Tu, Ty, self.Tmod[l]], writes=[Ty])
                s.op("dve", lambda e, c=c: e.scalar_tensor_tensor(out=y[:, c, :], in0=u[:, c, 2:N + 2], scalar=cw(2), in1=y[:, c, :], op0=ALU.mult, op1=ALU.add),
                     reads=[Tu, Ty, self.Tmod[l]], writes=[Ty])
                s.op("dve", lambda e, c=c: e.tensor_tensor(out=dt_[:, c, :], in0=sw[:, 1, c, 1:N + 1], in1=y[:, c, :], op=ALU.mult),
                     reads=[tsw, Ty], writes=[tdt_])
            s.dma("act", sc["DT"][:, i * N:(i + 1) * N].rearrange("(c p) t -> p c t", p=128), dt_[:], reads=[tdt_], writes=[TS["DT"]])
        s.barrier()


Prog.phase_poolconv = phase_poolconv


def phase3(self, l, kind):
    s = self.s
    sc = self.scr[kind]
    TS = sc["T"]
    lat = kind == "lat"
    N = TT if lat else CTX
    ntile = NTILE if lat else 1
    r = 0 if lat else 1
    M = self.mod[l]
    A = self.modA[l]
    last = (l == L_DEPTH - 1)
    if lat:
        src = self.xT if l == 0 else self.x1T
        Tsrc = self.Txin if l == 0 else self.Tx1
        dst, Tdst = (self.outT, self.Tout) if last else (self.x1T, self.Tx1)
    else:
        src = self.ctxT if l == 0 else self.xcT
        Tsrc = self.Txin if l == 0 else self.Txc
        dst, Tdst = self.xcT, self.Txc
    brs = [("w_pa", "AT", 8, 0), ("w_pb", "BT", 4, 8), ("w_pc", "CT", 4, 12), ("w_pd", "DT", 4, 16)]
    with contextlib.ExitStack() as st:
        self.alloc_wbufs(st)
        hT = self.sb(st, "p3_hT", [128, KC, N], BF16)
        xt = self.sb(st, "p3_xt", [128, KC, N], F32)
        big = self.sb(st, "p3_big", [128, FKC, N], BF16)
        yT = big[:, 0:KC, :]
        br = big[:, KC:KC + 20, :]
        acc = [self.sb(st, "p3_acc%d" % i, [128, N], F32) for i in range(4)]
        Tacc = [T() for _ in range(4)]
        sgp = Rot([(self.sb(st, "p3_sg%d" % i, [128, N], F32), T()) for i in range(3)])
        tmpp = Rot([(self.sb(st, "p3_tmp%d" % i, [128, N], F32), T()) for i in range(2)])
        sq = Rot([(self.sb(st, "p3_sq%d" % i, [128, N], BF16), T()) for i in range(2)])
        lnv = self.sb(st, "p3_ln", [128, N], F32)
        rstd = self.sb(st, "p3_rstd", [128, N], F32)
        ThT, Txt, TyT, Tbr, TaT, Tln = T(), T(), T(), T(), T(), T()
        for i in range(ntile):
            c0 = (HALO + i * TT) if lat else 0
            o0 = i * N
            s.dma("sp", hT[:], sc["HT"][:, o0:o0 + N].rearrange("(k p) t -> p k t", p=128), reads=[TS["HT"]], writes=[ThT])
            s.dma("sp", xt[:], src[:, c0:c0 + N].rearrange("(k p) t -> p k t", p=128), reads=[Tsrc], writes=[Txt])
            for (wn, bn, nk, off) in brs:
                s.dma("sp", br[:, off:off + nk, :], sc[bn][:, o0:o0 + N].rearrange("(k p) t -> p k t", p=128), reads=[TS[bn]], writes=[Tbr])
            for cb in range(4):
                for bi, (wn, bn, nk, off) in enumerate(brs):
                    wg, twg = self.wload("w_gate", (l * 4 + bi) * D, KC, cb * 512, 512)
                    wp, twp = self.wload(wn, l * nk * 128, nk, cb * 512, 512)
                    for dl in range(4):
                        psg, tpg = self.psum.get()
                        for k in range(KC):
                            s.op("pe", lambda e, k=k, dl=dl: e.matmul(psg[:, 0:N], lhsT=wg[:, k, dl * 128:(dl + 1) * 128], rhs=hT[:, k, :],
                                                                      start=(k == 0), stop=(k == KC - 1)), reads=[twg, ThT], writes=[tpg], pe_accum=True)
                        psp, tpp = self.psum.get()
                        for k in range(nk):
                            s.op("pe", lambda e, k=k, dl=dl: e.matmul(psp[:, 0:N], lhsT=wp[:, k, dl * 128:(dl + 1) * 128], rhs=br[:, off + k, :],
                                                                      start=(k == 0), stop=(k == nk - 1)), reads=[twp, Tbr], writes=[tpp], pe_accum=True)
                        sg, tsg = sgp.get()
                        s.op("act", lambda e: e.activation(out=sg[:], in_=psg[:, 0:N], func=AF.Sigmoid), reads=[tpg], writes=[tsg])
                        dc = cb * 4 + dl
                        if bi == 0:
                            s.op("dve", lambda e, dl=dl: e.tensor_tensor(out=acc[dl][:], in0=psp[:, 0:N], in1=sg[:], op=ALU.mult),
                                 reads=[tpp, tsg], writes=[Tacc[dl]])
                        else:
                            tm, ttm = tmpp.get()
                            s.op("dve", lambda e: e.tensor_tensor(out=tm[:], in0=psp[:, 0:N], in1=sg[:], op=ALU.mult), reads=[tpp, tsg], writes=[ttm])
                            if bi < 3:
                                s.op("dve", lambda e, dl=dl: e.tensor_tensor(out=acc[dl][:], in0=acc[dl][:], in1=tm[:], op=ALU.add),
                                     reads=[ttm, Tacc[dl]], writes=[Tacc[dl]])
                            else:
                                s.op("dve", lambda e, dl=dl, dc=dc: e.tensor_tensor(out=yT[:, dc, :], in0=acc[dl][:], in1=tm[:], op=ALU.add),
                                     reads=[ttm, Tacc[dl]], writes=[TyT])
            for cb in range(4):
                wo, two = self.wload("w_o", l * D, KC, cb * 512, 512)
                for dl in range(4):
                    dc = cb * 4 + dl
                    ps, tp = self.psum.get()
                    for k in range(KC):
                        s.op("pe", lambda e, k=k, dl=dl: e.matmul(ps[:, 0:N], lhsT=wo[:, k, dl * 128:(dl + 1) * 128], rhs=yT[:, k, :],
                                                                  start=(k == 0), stop=(k == KC - 1)), reads=[two, TyT], writes=[tp], pe_accum=True)
                    s.op("dve", lambda e, dc=dc: e.scalar_tensor_tensor(out=xt[:, dc, :], in0=ps[:, 0:N], scalar=M[:, r, 2, dc:dc + 1], in1=xt[:, dc, :],
                                                                        op0=ALU.mult, op1=ALU.add), reads=[tp, Txt, self.Tmod[l]], writes=[Txt])
            s.barrier()
            ps, tp = self.psum.get()
            for k in range(KC):
                q_, tq_ = sq.get()
                s.op("act", lambda e, k=k: e.activation(out=q_[:], in_=xt[:, k, :], func=AF.Square), reads=[Txt], writes=[tq_])
                s.op("pe", lambda e, k=k: e.matmul(ps[:, 0:N], lhsT=self.ones[:], rhs=q_[:], start=(k == 0), stop=(k == KC - 1)),
                     reads=[tq_, self.Tconst], writes=[tp], pe_accum=True)
            s.op("act", lambda e: e.activation(out=lnv[:], in_=ps[:, 0:N], func=AF.Ln, scale=1.0 / D, bias=EPS), reads=[tp], writes=[Tln])
            s.op("act", lambda e: e.activation(out=rstd[:], in_=lnv[:], func=AF.Exp, scale=-0.5), reads=[Tln], writes=[Tln])
            for k in range(KC):
                tm, ttm = tmpp.get()
                s.op("dve", lambda e, k=k: e.tensor_tensor(out=tm[:], in0=xt[:, k, :], in1=rstd[:], op=ALU.mult), reads=[Txt, Tln], writes=[ttm])
                s.op("act", lambda e, k=k: e.activation(out=hT[:, k, :], in_=tm[:], func=AF.Identity, scale=A[:, r, 1, k:k + 1], bias=M[:, r, 3, k:k + 1]),
                     reads=[ttm, self.Tmod[l]], writes=[ThT])
            for fb in range(DFF // 512):
                wg, twg = self.wload("w_ffn_gate", l * D, KC, fb * 512, 512)
                wu, twu = self.wload("w_ffn_up", l * D, KC, fb * 512, 512)
                for fl in range(4):
                    psg, tpg = self.psum.get()
                    for k in range(KC):
                        s.op("pe", lambda e, k=k, fl=fl: e.matmul(psg[:, 0:N], lhsT=wg[:, k, fl * 128:(fl + 1) * 128], rhs=hT[:, k, :],
                                                                  start=(k == 0), stop=(k == KC - 1)), reads=[twg, ThT], writes=[tpg], pe_accum=True)
                    psu, tpu = self.psum.get()
                    for k in range(KC):
                        s.op("pe", lambda e, k=k, fl=fl: e.matmul(psu[:, 0:N], lhsT=wu[:, k, fl * 128:(fl + 1) * 128], rhs=hT[:, k, :],
                                                                  start=(k == 0), stop=(k == KC - 1)), reads=[twu, ThT], writes=[tpu], pe_accum=True)
                    sg, tsg = sgp.get()
                    s.op("act", lambda e: e.activation(out=sg[:], in_=psg[:, 0:N], func=AF.Silu), reads=[tpg], writes=[tsg])
                    s.op("dve", lambda e, fb=fb, fl=fl: e.tensor_tensor(out=big[:, fb * 4 + fl, :], in0=psu[:, 0:N], in1=sg[:], op=ALU.mult),
                         reads=[tpu, tsg], writes=[TaT])
            for dc in range(KC):
                buf, tb = self.wbufs.get()
                wd = buf[:, 0:FKC * 128].rearrange("p (k n) -> p k n", n=128)
                s.dma("pool", wd, self.wsrc("w_ffn_down", l * DFF, DFF, dc * 128, 128).rearrange("(k p) n -> p k n", p=128),
                      reads=[self.TW["w_ffn_down"]], writes=[tb])
                ps, tp = self.psum.get()
                for k in range(FKC):
                    s.op("pe", lambda e, k=k: e.matmul(ps[:, 0:N], lhsT=wd[:, k, :], rhs=big[:, k, :], start=(k == 0), stop=(k == FKC - 1)),
                         reads=[tb, TaT], writes=[tp], pe_accum=True)
                s.op("dve", lambda e, dc=dc: e.scalar_tensor_tensor(out=xt[:, dc, :], in0=ps[:, 0:N], scalar=M[:, r, 5, dc:dc + 1], in1=xt[:, dc, :],
                                                                    op0=ALU.mult, op1=ALU.add), reads=[tp, Txt, self.Tmod[l]], writes=[Txt])
            if lat and not last:
                s.dma("act", dst[:, c0:c0 + N].rearrange("(k p) t -> p k t", p=128), xt[:], reads=[Txt], writes=[Tdst])
            else:
                s.dma("act", dst[:, o0:o0 + N].rearrange("(k p) t -> p k t", p=128), xt[:], reads=[Txt], writes=[Tdst])
            s.barrier()


Prog.phase3 = phase3


GROUPS4 = [list(range(NCORE))]


def phase_fft_lat(self, l):
    s = self.s
    nc = self.nc
    sc = self.scr["lat"]
    TS = sc["T"]
    uid = "l%d" % l
    ag1 = nc.dram_tensor("ag1_" + uid, [NCORE * 4 * OWN, 128], BF16, addr_space="Shared").ap()
    f2src = nc.dram_tensor("f2src_" + uid, [128, SEQ], BF16).ap()
    ag2 = nc.dram_tensor("ag2_" + uid, [NCORE * 128, SEQ], BF16, addr_space="Shared").ap()
    Tag1r, Tag1, Tf2, Tag2r, Tag2 = T(), T(), T(), T(), T()
    s.coll("AllGather", [sc["FT"].rearrange("g t c -> (g t) c")], [ag1[:, :]], GROUPS4, reads=[TS["FT"]], writes=[Tag1r])
    self.fence([Tag1r], [Tag1])
    with contextlib.ExitStack() as st:
        U = self.sb(st, "ff_U", [128, SEQ], BF16)
        B = self.sb(st, "ff_B", [128, 128, 2, 128], BF16)
        dft = self.sb(st, "ff_dft", [128, 4, 128], BF16)
        tw1 = self.sb(st, "ff_tw1", [128, 2, 128], F32)
        tw2 = self.sb(st, "ff_tw2", [128, 2, 128], F32)
        m1p = Rot([(self.sb(st, "ff_m1%d" % i, [128, 2, 2, 128], F32), T()) for i in range(2)])
        m2p = Rot([(self.sb(st, "ff_m2%d" % i, [128, 2, 2, 128], F32), T()) for i in range(2)])
        xbp = Rot([(self.sb(st, "ff_xb%d" % i, [128, 2, 4, 128], BF16), T()) for i in range(2)])
        Tc, TU, TB, TF = T(), T(), T(), T()
        s.dma("sp", dft[:], self.c_dft[:, :, :], writes=[Tc])
        s.dma("sp", tw1[:], self.c_tw[:, :, :], writes=[Tc])
        s.dma("sp", tw2[:, 0, :], self.c_tw[:, 1, :], writes=[Tc])
        s.dma("sp", tw2[:, 1, :], self.c_tw[:, 0, :], writes=[Tc])
        s.wait_all("pool", [self.Tconst])
        s.dma("pool", U[:], ag1.rearrange("(r t) c -> r (t c)", t=128), reads=[Tag1, self.Tconst], writes=[TU],
              indirect=bass.IndirectOffsetOnAxis(ap=self.gidx[:, 0:1], axis=0))
        Uv = U[:].rearrange("p (b c) -> p b c", c=128)
        for cp in range(64):
            ps, tp = self.psum.get()
            for ci in range(2):
                c = cp * 2 + ci
                s.op("pe", lambda e, c=c, ci=ci: e.matmul(ps[:, ci * 256:(ci + 1) * 256], lhsT=Uv[:, :, c], rhs=dft[:, 0:2, :], start=True, stop=True),
                     reads=[TU, Tc], writes=[tp], pe_accum=True)
            m1, tm1 = m1p.get()
            m2, tm2 = m2p.get()
            psv = ps[:, :].rearrange("p (c x) -> p c x", x=256)
            s.op("dve", lambda e: e.tensor_tensor(out=m1[:].rearrange("p c r q -> p c (r q)"), in0=psv,
                                                  in1=bmid(tw1[:].rearrange("p r q -> p (r q)"), 2), op=ALU.mult), reads=[tp, Tc], writes=[tm1])
            s.op("dve", lambda e: e.tensor_tensor(out=m2[:].rearrange("p c r q -> p c (r q)"), in0=psv,
                                                  in1=bmid(tw2[:].rearrange("p r q -> p (r q)"), 2), op=ALU.mult), reads=[tp, Tc], writes=[tm2])
            s.op("pool", lambda e, cp=cp: e.tensor_tensor(out=B[:, cp * 2:cp * 2 + 2, 0, :], in0=m1[:, :, 0, :], in1=m1[:, :, 1, :], op=ALU.subtract),
                 reads=[tm1], writes=[TB])
            s.op("pool", lambda e, cp=cp: e.tensor_tensor(out=B[:, cp * 2:cp * 2 + 2, 1, :], in0=m2[:, :, 0, :], in1=m2[:, :, 1, :], op=ALU.add),
                 reads=[tm2], writes=[TB])
        Fv = U[:].rearrange("p (x q) -> p x q", q=128)
        for qg in range(32):
            xb, txb = xbp.get()
            for half in range(2):
                ps, tp = self.psum.get()
                for qi in range(2):
                    q = qg * 4 + half * 2 + qi
                    s.op("pe", lambda e, q=q, qi=qi: e.matmul(ps[:, qi * 256:(qi + 1) * 256], lhsT=B[:, :, 0, q], rhs=dft[:, 0:2, :], start=True, stop=False),
                         reads=[TB, Tc], writes=[tp], pe_accum=True)
                    s.op("pe", lambda e, q=q, qi=qi: e.matmul(ps[:, qi * 256:(qi + 1) * 256], lhsT=B[:, :, 1, q], rhs=dft[:, 2:4, :], start=False, stop=True),
                         reads=[TB, Tc], writes=[tp], pe_accum=True)
                src_v = ps[:, :].rearrange("p (q r x) -> p r q x", q=2, r=2)
                if half == 0:
                    s.op("act", lambda e: e.activation(out=xb[:, :, 0:2, :], in_=src_v, func=AF.Copy), reads=[tp], writes=[txb])
                else:
                    s.op("dve", lambda e: e.tensor_copy(out=xb[:, :, 2:4, :], in_=src_v), reads=[tp], writes=[txb])
            ps3, tp3 = self.psum.get()
            s.op("pe", lambda e: e.matmul(ps3[:, :], lhsT=dft[:, 0, :], rhs=xb[:, 0].rearrange("p q x -> p (q x)"), start=True, stop=False),
                 reads=[txb, Tc], writes=[tp3], pe_accum=True)
            s.op("pe", lambda e: e.matmul(ps3[:, :], lhsT=dft[:, 2, :], rhs=xb[:, 1].rearrange("p q x -> p (q x)"), start=False, stop=True),
                 reads=[txb, Tc], writes=[tp3], pe_accum=True)
            s.op("act" if qg % 2 else "dve",
                 (lambda e: e.activation(out=Fv[:, :, qg * 4:qg * 4 + 4], in_=ps3[:, :].rearrange("p (q x) -> p x q", q=4), func=AF.Copy)) if qg % 2 else
                 (lambda e: e.tensor_copy(out=Fv[:, :, qg * 4:qg * 4 + 4], in_=ps3[:, :].rearrange("p (q x) -> p x q", q=4))),
                 reads=[tp3], writes=[TU])
        s.dma("sp", f2src[:, :], U[:], reads=[TU], writes=[Tf2])
        s.barrier()
    s.coll("AllGather", [f2src[:, :]], [ag2[:, :]], GROUPS4, reads=[Tf2], writes=[Tag2r])
    self.fence([Tag2r], [Tag2])
    with contextlib.ExitStack() as st:
        fb = self.sb(st, "ff_fb", [128, 4, OWN], BF16)
        Tfb = T()
        s.wait_all("pool", [self.Tconst])
        for g in range(4):
            s.dma("pool", fb[:, g, :], ag2.rearrange("r (j t) -> (r j) t", j=4), reads=[Tag2, self.Tconst], writes=[Tfb],
                  indirect=bass.IndirectOffsetOnAxis(ap=self.gidx[:, 1 + g:2 + g], axis=0))
        s.dma("sp", sc["BT"].rearrange("(g p) t -> p g t", p=128), fb[:], reads=[Tfb], writes=[TS["BT"]])
        s.barrier()


def phase_dft_ctx(self, l):
    s = self.s
    cc = self.scr["ctx"]
    TS = cc["T"]
    with contextlib.ExitStack() as st:
        U = self.sb(st, "fc_U", [128, 2, 4, 128], BF16)
        d256 = self.sb(st, "fc_d256", [128, 2, 2, 256], BF16)
        dft = self.sb(st, "fc_dft", [128, 4, 128], BF16)
        A = self.sb(st, "fc_A", [128, 2, 256], BF16)
        Fo = self.sb(st, "fc_F", [128, 4, CTX], BF16)
        Tc, TU, TA, TF = T(), T(), T(), T()
        s.dma("sp", dft[:], self.c_dft[:, :, :], writes=[Tc])
        s.dma("sp", d256[:], self.c_d256[:, :, :, :], writes=[Tc])
        for g in range(4):
            s.dma("sp", U[:, :, g, :], cc["FT"][g].rearrange("(s p) c -> p s c", p=128), reads=[TS["FT"]], writes=[TU])
        for g in range(4):
            ps, tp = self.psum.get()
            for lc in range(2):
                s.op("pe", lambda e, lc=lc, g=g: e.matmul(ps[:, :], lhsT=U[:, lc, g, :], rhs=d256[:, lc].rearrange("p r k -> p (r k)"),
                                                          start=(lc == 0), stop=(lc == 1)), reads=[TU, Tc], writes=[tp], pe_accum=True)
            s.op("act", lambda e: e.activation(out=A[:].rearrange("p r k -> p (r k)"), in_=ps[:, :], func=AF.Copy), reads=[tp], writes=[TA])
            ps2, tp2 = self.psum.get()
            s.op("pe", lambda e: e.matmul(ps2[:, 0:CTX], lhsT=dft[:, 0, :], rhs=A[:, 0, :], start=True, stop=False), reads=[TA, Tc], writes=[tp2], pe_accum=True)
            s.op("pe", lambda e: e.matmul(ps2[:, 0:CTX], lhsT=dft[:, 2, :], rhs=A[:, 1, :], start=False, stop=True), reads=[TA, Tc], writes=[tp2], pe_accum=True)
            s.op("dve", lambda e, g=g: e.tensor_copy(out=Fo[:, g, :], in_=ps2[:, 0:CTX]), reads=[tp2], writes=[TF])
        s.dma("act", cc["BT"].rearrange("(g p) t -> p g t", p=128), Fo[:], reads=[TF], writes=[TS["BT"]])
        s.barrier()


def phase_halo(self):
    s = self.s
    nc = self.nc
    hsrc = nc.dram_tensor("hsrc", [2 * D, HALO], F32).ap()
    ag3 = nc.dram_tensor("ag3", [NCORE * 2 * D, HALO], F32, addr_space="Shared").ap()
    Th, Tar, Ta = T(), T(), T()
    s.dma("sp", hsrc[0:D, :], self.x1T[:, HALO:2 * HALO], reads=[self.Tx1], writes=[Th])
    s.dma("sp", hsrc[D:2 * D, :], self.x1T[:, OWN:OWN + HALO], reads=[self.Tx1], writes=[Th])
    s.coll("AllGather", [hsrc[:, :]], [ag3[:, :]], GROUPS4, reads=[Th], writes=[Tar])
    self.fence([Tar], [Ta])
    with contextlib.ExitStack() as st:
        hb = self.sb(st, "hl_b", [128, 2, KC, HALO], F32)
        Thb = T()
        s.wait_all("pool", [self.Tconst])
        for side in range(2):
            for k in range(KC):
                col = 8 + side * 16 + k
                s.dma("pool", hb[:, side, k, :], ag3[:, :], reads=[Ta, self.Tconst], writes=[Thb],
                      indirect=bass.IndirectOffsetOnAxis(ap=self.gidx[:, col:col + 1], axis=0))
        s.dma("sp", self.x1T[:, 0:HALO].rearrange("(k p) t -> p k t", p=128), hb[:, 0], reads=[Thb], writes=[self.Tx1])
        s.dma("sp", self.x1T[:, HALO + OWN:EXT].rearrange("(k p) t -> p k t", p=128), hb[:, 1], reads=[Thb], writes=[self.Tx1])
        s.barrier()


Prog.phase_fft_lat = phase_fft_lat
Prog.phase_dft_ctx = phase_dft_ctx
Prog.phase_halo = phase_halo


_CACHE = {}


def kernel(**inputs):
    maps = prep_inputs(inputs)
    if "prog" not in _CACHE:
        _CACHE["prog"] = build()
    p = _CACHE["prog"]
    res = run(p, maps)
    out = np.zeros((2, SEQ, D), np.float32)
    for core in range(NCORE):
        b, j = core // 4, core % 4
        out[b, j * OWN:(j + 1) * OWN, :] = np.asarray(res.results[core]["outT"]).T
    return out
```
